# Optimizing a Trainium2 kernel written in Bass

```python
import jax
import jax.numpy as jnp
from jax import lax
import numpy as np

D_MODEL = 1024
BATCH = 8
SEQ = 2048
DEPTH = 4
DEC_BATCH = 32
DEC_SEQ = 4
PAST_LEN = 16384
PAGE_SIZE = 128

N_A_LAYERS = DEPTH // 2
N_B_LAYERS = DEPTH - N_A_LAYERS
POOL_WINDOWS = (2, 4, 8, 16)
N_POOL_GROUPS = len(POOL_WINDOWS)
POOL_WIDTH = D_MODEL
POOL_GROUP = POOL_WIDTH // N_POOL_GROUPS
POOL_BUF = max(POOL_WINDOWS) - 1
MLA_HEADS = 8
QK_NOPE = 128
QK_ROPE = 64
V_HEAD = 128
KV_RANK = 256
Q_RANK = 384
MLA_WIDTH = MLA_HEADS * V_HEAD
MLA_SCALE = (QK_NOPE + QK_ROPE) ** -0.5
ROPE_THETA = 10000.0
Q_BLOCK = 128
MEM_TOKENS = 256
MEM_HEADS = 4
MEM_HEAD_DIM = 128
MEM_WIDTH = MEM_HEADS * MEM_HEAD_DIM
MEM_SCALE = MEM_HEAD_DIM ** -0.5
IN_A = 2 * POOL_WIDTH + 2 * MEM_WIDTH
IN_B = Q_RANK + MLA_WIDTH + 2 * MEM_WIDTH
OUT_W = POOL_WIDTH + MEM_WIDTH
EPS = 1e-6

kernel_name = 'yoco_pool_mla_memory_decoder'


def rmsnorm(x, g):
    xf = x.astype(jnp.float32)
    y = xf * lax.rsqrt(jnp.mean(xf * xf, axis=-1, keepdims=True) + EPS)
    return (y * g.astype(jnp.float32)).astype(x.dtype)


def rope(x, pos):
    half = x.shape[-1] // 2
    inv = ROPE_THETA ** (-jnp.arange(half, dtype=jnp.float32) / half)
    ang = pos.astype(jnp.float32)[:, None] * inv[None, :]
    cos = jnp.cos(ang)[None, :, None, :]
    sin = jnp.sin(ang)[None, :, None, :]
    xf = x.astype(jnp.float32)
    x1, x2 = xf[..., :half], xf[..., half:]
    return jnp.concatenate([x1 * cos - x2 * sin, x1 * sin + x2 * cos], axis=-1).astype(x.dtype)


def pool_mix(u_ext, pos, w_grp, scale):
    B = u_ext.shape[0]
    T = pos.shape[0]
    uf = u_ext.astype(jnp.float32)
    cs = jnp.concatenate([jnp.zeros((B, 1, POOL_WIDTH), jnp.float32), jnp.cumsum(uf, axis=1)], axis=1)
    end = cs[:, POOL_BUF + 1:]
    cur = uf[:, POOL_BUF:]
    groups = []
    for g, w in enumerate(POOL_WINDOWS):
        c0, c1 = g * POOL_GROUP, (g + 1) * POOL_GROUP
        start = cs[:, POOL_BUF + 1 - w: POOL_BUF + 1 - w + T, c0:c1]
        cnt = jnp.minimum(pos + 1, w).astype(jnp.float32)[None, :, None]
        groups.append((end[..., c0:c1] - start) / cnt - cur[..., c0:c1])
    pooled = jnp.stack(groups, axis=2)
    mixed = jnp.einsum('btgc,gcd->btgd', pooled, w_grp.astype(jnp.float32))
    return (mixed.reshape(B, T, POOL_WIDTH) * scale.astype(jnp.float32)).astype(u_ext.dtype)


def mem_project(mem, g, w):
    B, M, _ = mem.shape
    return (rmsnorm(mem, g) @ w).reshape(B, M, MEM_HEADS, MEM_HEAD_DIM)


def mem_attend(q, mk, mv):
    B, T = q.shape[0], q.shape[1]
    s = jnp.einsum('bthd,bmhd->bhtm', q, mk).astype(jnp.float32) * MEM_SCALE
    p = jax.nn.softmax(s, axis=-1).astype(mv.dtype)
    return jnp.einsum('bhtm,bmhd->bthd', p, mv).reshape(B, T, MEM_WIDTH)


def shared_latent_kv(x, pos, g_kv_in, w_kv_down, g_kv_latent):
    kv = rmsnorm(x, g_kv_in) @ w_kv_down
    ckv = rmsnorm(kv[..., :KV_RANK], g_kv_latent)
    krope = rope(kv[..., KV_RANK:][:, :, None, :], pos)[:, :, 0, :]
    return ckv, krope


def mla_attend(q_lat, q_rope, ckv, krope, qpos, kpos):
    s = (jnp.einsum('bthc,bsc->bhts', q_lat, ckv).astype(jnp.float32)
         + jnp.einsum('bthr,bsr->bhts', q_rope, krope).astype(jnp.float32)) * MLA_SCALE
    mask = kpos[None, :] <= qpos[:, None]
    s = jnp.where(mask[None, None], s, -1e30)
    p = jax.nn.softmax(s, axis=-1).astype(ckv.dtype)
    return jnp.einsum('bhts,bsc->bthc', p, ckv)


def mla_mix(c_q, pos, ckv, krope, g_q, w_q_up, w_k_up, w_v_up):
    B, T, _ = c_q.shape
    q = (rmsnorm(c_q, g_q) @ w_q_up).reshape(B, T, MLA_HEADS, QK_NOPE + QK_ROPE)
    q_rope = rope(q[..., QK_NOPE:], pos)
    q_lat = jnp.einsum('bthd,chd->bthc', q[..., :QK_NOPE], w_k_up)
    kpos = jnp.arange(ckv.shape[1], dtype=jnp.int32)
    if T >= Q_BLOCK and T % Q_BLOCK == 0:
        nb = T // Q_BLOCK
        ql = q_lat.reshape(B, nb, Q_BLOCK, MLA_HEADS, KV_RANK).swapaxes(0, 1)
        qr = q_rope.reshape(B, nb, Q_BLOCK, MLA_HEADS, QK_ROPE).swapaxes(0, 1)
        qp = pos.reshape(nb, Q_BLOCK)
        o = lax.map(lambda a: mla_attend(a[0], a[1], ckv, krope, a[2], kpos), (ql, qr, qp))
        o_lat = o.swapaxes(0, 1).reshape(B, T, MLA_HEADS, KV_RANK)
    else:
        o_lat = mla_attend(q_lat, q_rope, ckv, krope, pos, kpos)
    return jnp.einsum('bthc,chd->bthd', o_lat, w_v_up).reshape(B, T, MLA_WIDTH)


def trunk(x, pos, pool_prev, ckv_past, krope_past, mem_k, mem_v,
          g_norm, w_in_a, w_pool_grp, pool_scale, w_in_b, g_q_latent, w_q_up,
          g_kv_in, w_kv_down, g_kv_latent, w_k_up, w_v_up, w_out, g_final):
    B, T, _ = x.shape
    pool_new = []
    ckv_new = krope_new = ckv_all = krope_all = None
    for l in range(DEPTH):
        if l == N_A_LAYERS:
            ckv_new, krope_new = shared_latent_kv(x, pos, g_kv_in, w_kv_down, g_kv_latent)
            if ckv_past is None:
                ckv_all, krope_all = ckv_new, krope_new
            else:
                ckv_all = jnp.concatenate([ckv_past, ckv_new], axis=1)
                krope_all = jnp.concatenate([krope_past, krope_new], axis=1)
        h = rmsnorm(x, g_norm[l])
        if l < N_A_LAYERS:
            z = h @ w_in_a[l]
            u, gate_t, q_m, gate_m = jnp.split(z, [POOL_WIDTH, 2 * POOL_WIDTH, 2 * POOL_WIDTH + MEM_WIDTH], axis=-1)
            prev = jnp.zeros((B, POOL_BUF, POOL_WIDTH), u.dtype) if pool_prev is None else pool_prev[l]
            u_ext = jnp.concatenate([prev, u], axis=1)
            pool_new.append(u_ext[:, -POOL_BUF:])
            tok = pool_mix(u_ext, pos, w_pool_grp[l], pool_scale[l])
        else:
            j = l - N_A_LAYERS
            z = h @ w_in_b[j]
            c_q, gate_t, q_m, gate_m = jnp.split(z, [Q_RANK, Q_RANK + MLA_WIDTH, Q_RANK + MLA_WIDTH + MEM_WIDTH], axis=-1)
            tok = mla_mix(c_q, pos, ckv_all, krope_all, g_q_latent[j], w_q_up[j], w_k_up, w_v_up)
        mem_o = mem_attend(q_m.reshape(B, T, MEM_HEADS, MEM_HEAD_DIM), mem_k[l], mem_v[l])
        mixed = jnp.concatenate([tok * jax.nn.silu(gate_t), mem_o * jax.nn.silu(gate_m)], axis=-1)
        x = x + mixed @ w_out[l]
    return rmsnorm(x, g_final), jnp.stack(pool_new, axis=0), ckv_new, krope_new


def setup_inputs(seed: int = 0) -> dict:
    key = jax.random.key(seed)
    ks = jax.random.split(key, 32)
    f32 = jnp.float32

    def nrm(k, shape, scale=1.0):
        return jax.random.normal(k, shape, f32) * scale

    def gain(k, shape):
        return 1.0 + 0.02 * jax.random.normal(k, shape, f32)

    n_pages = PAST_LEN // PAGE_SIZE
    n_used = DEC_BATCH * n_pages
    n_pool = n_used + max(1, n_used // 4)
    page_table = jax.random.permutation(ks[7], n_pool)[:n_used].reshape(DEC_BATCH, n_pages).astype(jnp.int32)
    return {
        'x_prompt': nrm(ks[0], (BATCH, SEQ, D_MODEL)),
        'x_sample': nrm(ks[1], (DEC_BATCH, DEC_SEQ, D_MODEL)),
        'state_pool': nrm(ks[2], (N_A_LAYERS, DEC_BATCH, POOL_BUF, POOL_WIDTH)),
        'cache_ckv': nrm(ks[3], (n_pool, PAGE_SIZE, KV_RANK)),
        'cache_krope': nrm(ks[4], (n_pool, PAGE_SIZE, QK_ROPE)),
        'cache_mem_k': nrm(ks[5], (DEPTH, DEC_BATCH, MEM_TOKENS, MEM_HEADS, MEM_HEAD_DIM)),
        'cache_mem_v': nrm(ks[6], (DEPTH, DEC_BATCH, MEM_TOKENS, MEM_HEADS, MEM_HEAD_DIM)),
        'page_table': page_table,
        'mem_prompt': nrm(ks[8], (BATCH, MEM_TOKENS, D_MODEL)),
        'g_norm': gain(ks[9], (DEPTH, D_MODEL)),
        'w_in_a': nrm(ks[10], (N_A_LAYERS, D_MODEL, IN_A), D_MODEL ** -0.5),
        'w_pool_grp': nrm(ks[11], (N_A_LAYERS, N_POOL_GROUPS, POOL_GROUP, POOL_GROUP), POOL_GROUP ** -0.5),
        'pool_scale': gain(ks[12], (N_A_LAYERS, POOL_WIDTH)),
        'w_in_b': nrm(ks[13], (N_B_LAYERS, D_MODEL, IN_B), D_MODEL ** -0.5),
        'g_q_latent': gain(ks[14], (N_B_LAYERS, Q_RANK)),
        'w_q_up': nrm(ks[15], (N_B_LAYERS, Q_RANK, MLA_HEADS * (QK_NOPE + QK_ROPE)), Q_RANK ** -0.5),
        'g_kv_in': gain(ks[16], (D_MODEL,)),
        'w_kv_down': nrm(ks[17], (D_MODEL, KV_RANK + QK_ROPE), D_MODEL ** -0.5),
        'g_kv_latent': gain(ks[18], (KV_RANK,)),
        'w_k_up': nrm(ks[19], (KV_RANK, MLA_HEADS, QK_NOPE), KV_RANK ** -0.5),
        'w_v_up': nrm(ks[20], (KV_RANK, MLA_HEADS, V_HEAD), KV_RANK ** -0.5),
        'g_mem': gain(ks[21], (DEPTH, D_MODEL)),
        'w_mem_k': nrm(ks[22], (DEPTH, D_MODEL, MEM_WIDTH), D_MODEL ** -0.5),
        'w_mem_v': nrm(ks[23], (DEPTH, D_MODEL, MEM_WIDTH), D_MODEL ** -0.5),
        'w_out': nrm(ks[24], (DEPTH, OUT_W, D_MODEL), OUT_W ** -0.5),
        'g_final': gain(ks[25], (D_MODEL,)),
    }


def reference(x_prompt, x_sample, state_pool, cache_ckv, cache_krope, cache_mem_k, cache_mem_v,
              page_table, mem_prompt, g_norm, w_in_a, w_pool_grp, pool_scale, w_in_b, g_q_latent,
              w_q_up, g_kv_in, w_kv_down, g_kv_latent, w_k_up, w_v_up, g_mem, w_mem_k, w_mem_v,
              w_out, g_final):
    weights = (g_norm, w_in_a, w_pool_grp, pool_scale, w_in_b, g_q_latent, w_q_up,
               g_kv_in, w_kv_down, g_kv_latent, w_k_up, w_v_up, w_out, g_final)
    pos_p = jnp.arange(x_prompt.shape[1], dtype=jnp.int32)
    mem_k_p = jnp.stack([mem_project(mem_prompt, g_mem[l], w_mem_k[l]) for l in range(DEPTH)], axis=0)
    mem_v_p = jnp.stack([mem_project(mem_prompt, g_mem[l], w_mem_v[l]) for l in range(DEPTH)], axis=0)
    y_p, pool_p, ckv_p, krope_p = trunk(x_prompt, pos_p, None, None, None, mem_k_p, mem_v_p, *weights)
    db, n_pages = page_table.shape
    past = n_pages * cache_ckv.shape[1]
    ckv_past = cache_ckv[page_table].reshape(db, past, KV_RANK)
    krope_past = cache_krope[page_table].reshape(db, past, QK_ROPE)
    pos_s = past + jnp.arange(x_sample.shape[1], dtype=jnp.int32)
    y_s, pool_s, ckv_s, krope_s = trunk(x_sample, pos_s, state_pool, ckv_past, krope_past,
                                        cache_mem_k, cache_mem_v, *weights)
    return (y_p, y_s, pool_p, pool_s, ckv_p, krope_p, ckv_s, krope_s, mem_k_p, mem_v_p)
```

```python
import numpy as np
from contextlib import ExitStack
import concourse.bass as bass
import concourse.mybir as mybir
from concourse.bass_utils import run_bass_kernel_spmd

F32 = mybir.dt.float32
BF16 = mybir.dt.bfloat16
I32 = mybir.dt.int32
AF = mybir.ActivationFunctionType
ALU = mybir.AluOpType

NCORES = 8
D = 1024
SEQ = 2048
HALF = 1024
NSB = 4
NST = 4
NS = NSB * NST
NPG = 128
EPS = 1e-6
MEM_SCALE = 128 ** -0.5
MLA_SCALE = 192 ** -0.5
POOL_W = (2, 4, 8, 16)


class Prog:
    def __init__(self, nc, es):
        self.nc = nc
        self.E = {'pe': nc.tensor, 'act': nc.scalar, 'dve': nc.vector, 'pool': nc.gpsimd, 'sp': nc.sync}
        self.semh = {e: es.enter_context(nc.semaphore('sem_' + e)) for e in self.E}
        self.cnt = {e: 0 for e in self.E}
        self.waited = {e: {} for e in self.E}
        self.lastw = {}
        self.readers = {}
        ND = 24
        self.dq = {'sp': ['h%d' % i for i in range(ND)], 'pool': ['w%d' % i for i in range(ND)]}
        self.dnames = self.dq['sp'] + self.dq['pool']
        for n in self.dnames:
            self.semh[n] = es.enter_context(nc.semaphore('sem_' + n))
        self.dval = {n: 0 for n in self.dnames}
        self.di = {'sp': 0, 'pool': 0}

    def _wait_for(self, e, toks):
        need = {}
        for (sn, v) in toks:
            if v > need.get(sn, 0):
                need[sn] = v
        for sn, v in need.items():
            if self.waited[e].get(sn, 0) >= v:
                continue
            self.E[e].wait_ge(self.semh[sn], v)
            self.waited[e][sn] = v

    def _deps(self, e, R, W):
        toks = []
        for k in R:
            t = self.lastw.get(k)
            if t is not None and not (e == 'pe' and t[0] == 'pe'):
                toks.append(t)
            if k == 'psB' or k == 'psB1' or (isinstance(k, tuple) and k[0] == 'ps'):
                for sn, v in self.readers.get(k, {}).items():
                    if sn != e:
                        toks.append((sn, v))
        for k in W:
            t = self.lastw.get(k)
            if t is not None and not (e == 'pe' and t[0] == 'pe'):
                toks.append(t)
            ks = [k]
            if isinstance(k, tuple) and isinstance(k[0], tuple):
                ks.append(('slotreaders', k[0]))
            for kk in ks:
                for sn, v in self.readers.get(kk, {}).items():
                    if not (e == 'pe' and sn == 'pe'):
                        toks.append((sn, v))
        return toks

    def _commit(self, R, W, tok):
        for k in R:
            ks = [k]
            if isinstance(k, tuple) and isinstance(k[0], tuple):
                ks.append(('slotreaders', k[0]))
            for kk in ks:
                d = self.readers.setdefault(kk, {})
                if d.get(tok[0], 0) < tok[1]:
                    d[tok[0]] = tok[1]
        for k in W:
            self.lastw[k] = tok
            self.readers[k] = {}

    def op(self, e, fn, R=(), W=()):
        self._wait_for(e, self._deps(e, R, W))
        ins = fn(self.E[e])
        self.cnt[e] += 1
        ins.then_inc(self.semh[e], 1)
        self._commit(R, W, (e, self.cnt[e]))

    def dma(self, e, out, in_, R=(), W=(), indirect=None):
        toks = self._deps(None, R, W)
        n = self.dq[e][self.di[e]]
        self.di[e] = (self.di[e] + 1) % len(self.dq[e])
        toks.append((n, self.dval[n]))
        self._wait_for(e, toks)
        if indirect is None:
            ins = self.E[e].dma_start(out=out, in_=in_)
        else:
            ins = self.E[e].indirect_dma_start(out=out, out_offset=None, in_=in_, in_offset=indirect)
        self.dval[n] += 16
        ins.then_inc(self.semh[n], 16)
        self._commit(R, W, (n, self.dval[n]))

    def barrier(self):
        toks = [(e, self.cnt[e]) for e in self.E] + [(n, self.dval[n]) for n in self.dnames]
        for e in self.E:
            self._wait_for(e, [t for t in toks if t[0] != e])

    def finish(self):
        toks = [(e, self.cnt[e]) for e in self.E if e != 'sp'] + [(n, self.dval[n]) for n in self.dnames]
        self._wait_for('sp', toks)


class _Stop(Exception):
    pass


STOP_AT = None
CACHE_ROWS = 5120 * 128


def build():
    nc = bass.Bass("TRN2", target_bir_lowering=False, dynamic_dma_scratch_size=8192)
    es = ExitStack()

    def din(name, shape, dt=F32):
        return nc.dram_tensor(name, list(shape), dt, kind="ExternalInput").ap()

    def dout(name, shape):
        return nc.dram_tensor(name, list(shape), F32, kind="ExternalOutput").ap()

    x_d = din("x", [SEQ, D]); xs_d = din("xs", [NS, D]); sp_d = din("sp", [2, NSB * 15, D])
    cache_d = din("cache", [CACHE_ROWS, 320])
    cmk_d = din("cmk", [4, NSB, 256, 512]); cmv_d = din("cmv", [4, NSB, 256, 512])
    pt_d = din("pt", [1, NSB * NPG], I32); memp_d = din("memp", [256, D])
    gfm_d = din("gfm", [128, 96])
    wina_d = din("wina", [2, D, 3072]); wgrp_d = din("wgrp", [2, 4, 256, 256])
    winb_d = din("winb", [2, D, 2432]); wqup_d = din("wqup", [2, 384, 1536])
    wkv_d = din("wkv", [D, 320]); glat_d = din("glat", [1, 256]); gfin_d = din("gfin", [1, D])
    wkT_d = din("wkT", [128, 8, 256]); wv_d = din("wv", [256, 1024])
    wmk_d = din("wmk", [4, D, 512]); wmv_d = din("wmv", [4, D, 512]); wout_d = din("wout", [4, 1536, D])
    identf_d = din("identf", [128, 128]); tri_d = din("tri", [128, 128]); tri4_d = din("tri4", [4, 32])
    cosq_d = din("cosq", [64, SEQ + NS]); sinq_d = din("sinq", [64, SEQ + NS])
    cosk_d = din("cosk", [128, 16, 32]); sink_d = din("sink", [128, 16, 32])
    cosks_d = din("cosks", [4, 32]); sinks_d = din("sinks", [4, 32])
    rc_d = din("rc", [128, 4, 16]); iop_d = din("iop", [128, 1])

    y_d = dout("y", [SEQ, D]); ys_d = dout("ys", [NS, D])
    poolp_d = dout("poolp", [2, 15, D]); pools_d = dout("pools", [2, NSB, 15, D])
    ckvp_d = dout("ckvp", [SEQ, 256]); krp_d = dout("krp", [SEQ, 64])
    ckvs_d = dout("ckvs", [NS, 256]); krs_d = dout("krs", [NS, 64])
    mkp_d = dout("mkp", [4, 256, 512]); mvp_d = dout("mvp", [4, 256, 512])

    P = Prog(nc, es)

    def stage(name):
        if STOP_AT is not None and name == STOP_AT:
            raise _Stop()
    sbytes = [0]
    uid = [0]

    def sb(name, shape, dt, stack=es):
        n = 1
        for s in shape[1:]:
            n *= s
        sbytes[0] += n * (4 if dt in (F32, I32) else 2)
        uid[0] += 1
        return stack.enter_context(nc.sbuf_tensor('s%d_%s' % (uid[0], name), list(shape), dt))

    NH = HALF + NS
    xT = sb("xT", [128, 8, NH], F32)
    hT = sb("hT", [128, 8, NH], BF16)
    mixT = sb("mixT", [128, 12, NH], BF16)
    ckvT = sb("ckvT", [128, 2, SEQ], BF16)
    krT = sb("krT", [64, SEQ], BF16)
    cnat = sb("cnat", [128, 16, 256], BF16)
    rstd = sb("rstd", [128, NH], F32)
    sqb = sb("sqb", [128, 2, 512], BF16)
    wsl = sb("wsl", [128, 2, 4096], BF16)
    wg2 = sb("wg2", [128, 2, 512], BF16)
    wmem = sb("wmem", [128, 2, 2048], BF16)
    tokbuf = sb("tokbuf", [128, 2, 1024], F32)
    mkT_all = sb("mkT_all", [128, 4, 4, 256], BF16)
    mv_all = sb("mv_all", [128, 4, 2, 512], BF16)
    mkTs = sb("mkTs", [128, 1, 4, 256], BF16)
    mvs = sb("mvs", [128, 1, 2, 512], BF16)
    qmT = sb("qmT", [128, 512], BF16); sgm = sb("sgm", [128, 512], BF16)
    qmTs = sb("qmTs", [128, 4, NS], BF16); sgms = sb("sgms", [128, 4, NS], BF16)
    pTm = sb("pTm", [128, 2, 512], BF16)
    rden = sb("rden", [128, 512], F32); tsb = sb("tsb", [128, 512], F32)
    identf = sb("identf", [128, 128], F32); identb = sb("identb", [128, 128], BF16)
    onesb = sb("onesb", [128, 128], BF16); tri = sb("tri", [128, 128], BF16); tri4 = sb("tri4", [4, 32], BF16)
    gfm = sb("gfm", [128, 96], F32)
    glat = sb("glat", [128, 256], F32)
    halo = sb("halo", [128, 2, 8, 15], F32)
    small = sb("small", [128, 8], F32)
    idx = sb("idx", [128, NSB * NPG], I32); iop = sb("iop", [128, 1], F32)
    ckvT_n = sb("ckvT_n", [128, 2, NSB, NST], BF16); krT_n = sb("krT_n", [64, NSB, NST], BF16)
    cnat_n = sb("cnat_n", [4, NSB, 257], BF16)

    psF = es.enter_context(nc.psum_tensor("psF", [128, 7, 512], F32))
    psB = es.enter_context(nc.psum_tensor("psB", [128, 1024], BF16))

    def bank(b):
        return psF[:, b, :]

    rot = {}

    def nxt(name, n):
        v = rot.get(name, 0)
        rot[name] = (v + 1) % n
        return v

    def sbank():
        b = nxt('sbank', 3)
        return b, ('ps', b)

    def mm(out, pairs, R, W, start=True, stop=True):
        def fn(e):
            ins = None
            for i, (l, r) in enumerate(pairs):
                ins = e.matmul(out, l, r, start=(start and i == 0), stop=(stop and i == len(pairs) - 1))
            return ins
        P.op('pe', fn, R, W)

    def tp(out, in_, ident, R, W):
        P.op('pe', lambda e: e.transpose(out, in_, ident), R, W)

    def act(out, in_, func, R, W, scale=1.0, bias=0.0, accum=None):
        if accum is None:
            P.op('act', lambda e: e.activation(out=out, in_=in_, func=func, bias=bias, scale=scale), R, W)
        else:
            P.op('act', lambda e: e.activation(out=out, in_=in_, func=func, bias=bias, scale=scale, accum_out=accum), R, W)

    def tt(out, a, b, op, R, W, eng='dve'):
        P.op(eng, lambda e: e.tensor_tensor(out=out, in0=a, in1=b, op=op), R, W)

    def stt(out, a, s, b, op0, op1, R, W):
        P.op('dve', lambda e: e.scalar_tensor_tensor(out=out, in0=a, scalar=s, in1=b, op0=op0, op1=op1), R, W)

    def cp(out, in_, R, W, eng='dve'):
        if eng == 'act':
            P.op('act', lambda e: e.copy(out=out, in_=in_), R, W)
        else:
            P.op(eng, lambda e: e.tensor_copy(out=out, in_=in_), R, W)

    def recip(out, in_, R, W):
        P.op('dve', lambda e: e.reciprocal(out=out, in_=in_), R, W)

    def memset(ap, v, W, eng='dve'):
        P.op(eng, lambda e: e.memset(ap, v), (), W)

    def wslot():
        s = nxt('wsl', 2)
        return s, ('wsl', s)

    def loadw(dst, src, key, part):
        P.dma('pool', dst, src, R=(), W=[(key, part)])

    try:
        P.dma('sp', identf[:], identf_d[:], W=['identf'])
        P.dma('pool', identb[:], identf_d[:], W=['identb'])
        P.dma('pool', tri[:], tri_d[:], W=['tri'])
        P.dma('pool', tri4[:], tri4_d[:], W=['tri4'])
        P.dma('sp', gfm[:], gfm_d[:], W=['gfm'])
        P.dma('sp', glat[:], glat_d[0:1, :].to_broadcast([128, 256]), W=['glat'])
        P.dma('sp', iop[:], iop_d[:], W=['iop'])
        memset(onesb[:], 1.0, ['onesb'])
        memset(cnat_n[:], 1.0, ['cnat_n'])
        with ExitStack() as ss_:
            ptb = sb("ptb", [128, NSB * NPG], I32, ss_)
            P.dma('sp', ptb[:], pt_d[0:1, :].to_broadcast([128, NSB * NPG]), W=['ptb'])
            P.op('dve', lambda e: e.tensor_scalar(out=idx[:], in0=ptb[:], scalar1=128.0, scalar2=iop[:, 0:1],
                                                 op0=ALU.mult, op1=ALU.add), ['ptb', 'iop'], ['idx'])
            P.barrier()
        COSK = ['cosk', 'sink', 'cosks', 'sinks']
        G_NORM, G_KV, G_MEM, G_PS, G_Q = 0, 32, 40, 72, 88

        for l in range(2):
            P.dma('sp', pools_d[l][:, 0:11, :], sp_d[l].rearrange("(b r) d -> b r d", r=15)[:, 4:15, :], W=[('pools_prev', l)])

        def load_sprev(sprevT):
          for l in range(2):
            P.dma('sp', tokbuf[0:60, l, :], sp_d[l], W=[('tokbuf', l)])
            for q in range(2):
                b_, bk = sbank()
                for k4 in range(4):
                    kc = q * 4 + k4
                    tp(bank(b_)[:, k4 * 60:(k4 + 1) * 60], tokbuf[0:60, l, kc * 128:(kc + 1) * 128], identf[0:60, 0:60],
                       [('tokbuf', l), 'identf'], [bk])
                cp(sprevT[:, l, q * 4:(q + 1) * 4, :], bank(b_)[:, 0:240].rearrange("p (a b) -> p a b", b=60), [bk], ['sprevT'])

        stage('setup')
        with ExitStack() as ms:
            memhT = sb("memhT", [128, 8, 256], BF16, ms)
            memn = sb("memn", [128, 8, 256], BF16, ms)
            memb = sb("memb", [128, 2, 1024], BF16, ms)
            for mt in range(2):
                P.dma('sp', tokbuf[:, mt, :], memp_d[mt * 128:(mt + 1) * 128, :], W=[('tokbuf', mt)])
                act(memb[:, mt, :], tokbuf[:, mt, :], AF.Square, [('tokbuf', mt)], [('memb', mt), ('small', mt)], accum=small[:, mt:mt + 1])
                act(small[:, 2 + mt:3 + mt], small[:, mt:mt + 1], AF.Ln, [('small', mt)], [('small', 2 + mt)], scale=1.0 / D, bias=EPS)
                act(small[:, 4 + mt:5 + mt], small[:, 2 + mt:3 + mt], AF.Exp, [('small', 2 + mt)], [('small', 4 + mt)], scale=-0.5)
                stage('mem_a')
                P.op('dve', lambda e, mt=mt: e.tensor_scalar(out=memb[:, mt, :], in0=tokbuf[:, mt, :], scalar1=small[:, 4 + mt:5 + mt],
                                                            scalar2=None, op0=ALU.mult), [('tokbuf', mt), ('small', 4 + mt)], [('memb', mt)])
                stage('mem_b')
                for q in range(2):
                    for k4 in range(4):
                        kc = q * 4 + k4
                        tp(psB[:, k4 * 128:(k4 + 1) * 128], memb[:, mt, kc * 128:(kc + 1) * 128], identb[:], [('memb', mt), 'identb'], ['psB'])
                    cp(memhT[:, q * 4:(q + 1) * 4, mt * 128:(mt + 1) * 128], psB[:, 0:512].rearrange("p (a b) -> p a b", b=128), ['psB'], ['memhT'])
                    stage('mem_c')
            for l in range(4):
                for kc in range(8):
                    P.op('dve', lambda e, kc=kc, l=l: e.tensor_scalar(out=memn[:, kc, :], in0=memhT[:, kc, :],
                                                                     scalar1=gfm[:, G_MEM + l * 8 + kc:G_MEM + l * 8 + kc + 1], scalar2=None, op0=ALU.mult),
                         ['memhT', 'gfm'], ['memn'])
                stage('mem_d')
                for which, wd, od in ((0, wmk_d, mkp_d), (1, wmv_d, mvp_d)):
                    s, sk = wslot()
                    W_ = wsl[:, s, :].rearrange("p (k n) -> p k n", n=512)
                    loadw(W_, wd[l].rearrange("(k p) n -> p k n", p=128), sk, 0)
                    stage('mem_e')
                    for mt in range(2):
                        b_, bk = sbank()
                        mm(bank(b_), [(memn[:, kc, mt * 128:(mt + 1) * 128], W_[:, kc, :]) for kc in range(8)], ['memn', (sk, 0)], [bk])
                        stage('mem_f')
                        cp(tokbuf[:, mt, 0:512], bank(b_), [bk], [('tokbuf', mt)])
                        if which == 1:
                            cp(mv_all[:, l, mt, :], bank(b_), [bk], [('mv_all', l)], eng='act')
                        P.dma('sp', od[l, mt * 128:(mt + 1) * 128, :], tokbuf[:, mt, 0:512], R=[('tokbuf', mt)], W=[('memout', which, l, mt)])
                        stage('mem_o%d_%d_%d' % (l, which, mt))
                        stage('mem_g')
                    if which == 0:
                        for h in range(4):
                            b_, bk = sbank()
                            mm(bank(b_)[:, 0:256], [(W_[:, kc, h * 128:(h + 1) * 128], memn[:, kc, :]) for kc in range(8)], ['memn', (sk, 0)], [bk])
                            cp(mkT_all[:, l, h, :], bank(b_)[:, 0:256], [bk], [('mkT_all', l)], eng='act')
                            stage('mem_h%d_%d' % (l, h))
        P.barrier()

        stage('mem')
        def chunks(hf):
            c = [(0, 512), (512, 512)]
            if hf == 1:
                c.append((1024, NS))
            return c

        def norm_rstd_chunk(c0, n):
            b_, bk = sbank()
            for kc in range(8):
                s = nxt('sq', 2)
                act(sqb[:, s, 0:n], xT[:, kc, c0:c0 + n], AF.Square, ['xT'], [('sqb', s)])
                mm(bank(b_)[:, 0:n], [(onesb[:], sqb[:, s, 0:n])], [('sqb', s), 'onesb'], [bk], start=(kc == 0), stop=(kc == 7))
            act(rstd[:, c0:c0 + n], bank(b_)[:, 0:n], AF.Ln, [bk], [('rstd', c0)], scale=1.0 / D, bias=EPS)
            act(rstd[:, c0:c0 + n], rstd[:, c0:c0 + n], AF.Exp, [('rstd', c0)], [('rstd', c0)], scale=-0.5)

        def make_h_chunk(c0, n, gcol):
            for kc in range(8):
                stt(hT[:, kc, c0:c0 + n], xT[:, kc, c0:c0 + n], gfm[:, gcol + kc:gcol + kc + 1], rstd[:, c0:c0 + n],
                    ALU.mult, ALU.mult, ['xT', ('rstd', c0), 'gfm'], [('hT', c0)])

        def norm_h(hf, gcol, do_rstd=True):
            for (c0, n) in chunks(hf):
                if do_rstd:
                    norm_rstd_chunk(c0, n)
                make_h_chunk(c0, n, gcol)

        def w_in_view(l):
            if l < 2:
                return wina_d[l].rearrange("(k p) n -> p k n", p=128)
            return winb_d[l - 2].rearrange("(k p) n -> p k n", p=128)

        def mem_attn_gen(hf, l, qb, gb):
            wv_ = w_in_view(l)
            for h in range(4):
                s = nxt('wmem', 2); sk = ('wmem', s)
                W_ = wmem[:, s, :].rearrange("p (k n) -> p k n", n=256)
                loadw(W_[:, :, 0:128], wv_[:, :, qb + h * 128:qb + (h + 1) * 128], sk, 0)
                loadw(W_[:, :, 128:256], wv_[:, :, gb + h * 128:gb + (h + 1) * 128], sk, 1)
                WR = [(sk, 0), (sk, 1)]
                for (c0, n) in chunks(hf):
                    samp = (n == NS)
                    b_, bk = sbank()
                    mm(bank(b_)[:, 0:n], [(W_[:, kc, 0:128], hT[:, kc, c0:c0 + n]) for kc in range(8)], WR + [('hT', c0)], [bk])
                    if samp:
                        cp(qmTs[:, h, :], bank(b_)[:, 0:n], [bk], ['qmTs'], eng='act')
                    else:
                        cp(qmT[:, 0:n], bank(b_)[:, 0:n], [bk], ['qmT'], eng='act')
                    yield
                    b2, bk2 = sbank()
                    mm(bank(b2)[:, 0:n], [(W_[:, kc, 128:256], hT[:, kc, c0:c0 + n]) for kc in range(8)], WR + [('hT', c0)], [bk2])
                    if samp:
                        act(sgms[:, h, :], bank(b2)[:, 0:n], AF.Silu, [bk2], ['sgms'])
                        yield
                        continue
                    act(sgm[:, 0:n], bank(b2)[:, 0:n], AF.Silu, [bk2], ['sgm'])
                    yield
                    for mt in range(2):
                        b3, bk3 = sbank()
                        mm(bank(b3), [(mkT_all[:, l, h, mt * 128:(mt + 1) * 128], qmT[:])], [('mkT_all', l), 'qmT'], [bk3])
                        act(pTm[:, mt, :], bank(b3), AF.Exp, [bk3], [('pTm', mt)], scale=MEM_SCALE)
                        yield
                    bo, bko = sbank()
                    bd, bkd = sbank()
                    mm(bank(bo), [(mv_all[:, l, mt, h * 128:(h + 1) * 128], pTm[:, mt, :]) for mt in range(2)],
                       [('mv_all', l), ('pTm', 0), ('pTm', 1)], [bko])
                    mm(bank(bd), [(onesb[:], pTm[:, mt, :]) for mt in range(2)], ['onesb', ('pTm', 0), ('pTm', 1)], [bkd])
                    act(rden[:], bank(bd), AF.Ln, [bkd], ['rden'])
                    act(rden[:], rden[:], AF.Exp, ['rden'], ['rden'], scale=-1.0)
                    tt(tsb[:], rden[:], sgm[:], ALU.mult, ['rden', 'sgm'], ['tsb'])
                    tt(mixT[:, 8 + h, c0:c0 + n], bank(bo), tsb[:], ALU.mult, [bko, 'tsb'], [('mixT', c0)])
                    yield

        def mem_attn_samp(hf, l):
            if hf == 1:
                for b in range(NSB):
                    s = 0
                    for mt in range(2):
                        P.dma('sp', tokbuf[:, mt, 0:512], cmk_d[l, b, mt * 128:(mt + 1) * 128, :], W=[('tokbuf', mt)])
                    P.dma('pool', mvs[:, s, :, :], cmv_d[l, b].rearrange("(m p) n -> p m n", p=128), W=[('mvs', s)])
                    for h in range(4):
                        b_, bk = sbank()
                        for mt in range(2):
                            tp(bank(b_)[:, mt * 128:(mt + 1) * 128], tokbuf[:, mt, h * 128:(h + 1) * 128], identf[:],
                               [('tokbuf', mt), 'identf'], [bk])
                        cp(mkTs[:, s, h, :], bank(b_)[:, 0:256], [bk], [('mkTs', s)], eng=('act' if h % 2 else 'dve'))
                    for h in range(4):
                        b3, bk3 = sbank()
                        for mt in range(2):
                            mm(bank(b3)[:, mt * 4:(mt + 1) * 4], [(mkTs[:, s, h, mt * 128:(mt + 1) * 128], qmTs[:, h, b * 4:(b + 1) * 4])],
                               [('mkTs', s), 'qmTs'], [bk3])
                        act(pTm[:, 0, 0:8], bank(b3)[:, 0:8], AF.Exp, [bk3], [('pTm', 0)], scale=MEM_SCALE)
                        mm(bank(4)[:, 0:4], [(mvs[:, s, mt, h * 128:(h + 1) * 128], pTm[:, 0, mt * 4:(mt + 1) * 4]) for mt in range(2)],
                           [('mvs', s), ('pTm', 0)], [('ps', 4)])
                        mm(bank(5)[:, 0:4], [(onesb[:], pTm[:, 0, mt * 4:(mt + 1) * 4]) for mt in range(2)], ['onesb', ('pTm', 0)], [('ps', 5)])
                        recip(rden[:, 0:4], bank(5)[:, 0:4], [('ps', 5)], ['rden'])
                        tt(tsb[:, 0:4], rden[:, 0:4], sgms[:, h, b * 4:(b + 1) * 4], ALU.mult, ['rden', 'sgms'], ['tsb'])
                        tt(mixT[:, 8 + h, HALF + b * 4:HALF + (b + 1) * 4], bank(4)[:, 0:4], tsb[:, 0:4], ALU.mult,
                           [('ps', 4), 'tsb'], [('mixT', HALF)])

        def w_out(hf, l):
            wo = wout_d[l].rearrange("(k p) d -> p k d", p=128)
            for dp in range(4):
                s, sk = wslot()
                W_ = wsl[:, s, 0:3072].rearrange("p (k n) -> p k n", n=256)
                loadw(W_, wo[:, :, dp * 256:(dp + 1) * 256], sk, 0)
                for dcc in range(2):
                    dc = dp * 2 + dcc
                    for (c0, n) in chunks(hf):
                        b_, bk = sbank()
                        mm(bank(b_)[:, 0:n], [(W_[:, kc, dcc * 128:(dcc + 1) * 128], mixT[:, kc, c0:c0 + n]) for kc in range(12)],
                           [(sk, 0), ('mixT', c0)], [bk])
                        tt(xT[:, dc, c0:c0 + n], xT[:, dc, c0:c0 + n], bank(b_)[:, 0:n], ALU.add, ['xT', bk], ['xT'])

        def layer_a(hf, l, A):
            uext2, tmpA, tmpB, pooled, sg2, usc2, tfix, sprevT = A
            pstP = tokbuf[:, 0, :]; pstS = tokbuf[:, 1, :]
            norm_h(hf, G_NORM + l * 8)
            wv_ = w_in_view(l)
            WE = 1039 if hf == 0 else 1039 + NSB * 19
            SB_ = 1039
            mgen = mem_attn_gen(hf, l, 2048, 2560)

            def bgstep(k=1):
                for _ in range(k):
                    next(mgen, None)
            ginfo = {}

            def emit_U(g):
                sl = g % 2
                uext = uext2[:, sl]; sg = sg2[:, sl]; usc = usc2[:, sl]
                s, sk = wslot()
                W_ = wsl[:, s, :].rearrange("p (k n) -> p k n", n=512)
                loadw(W_[:, :, 0:256], wv_[:, :, g * 256:(g + 1) * 256], sk, 0)
                loadw(W_[:, :, 256:512], wv_[:, :, 1024 + g * 256:1024 + (g + 1) * 256], sk, 1)
                s2 = nxt('wg2', 2); sk2 = ('wg2', s2)
                Wg = wg2[:, s2, :].rearrange("p (k n) -> p k n", n=256)
                loadw(Wg, wgrp_d[l, g].rearrange("(k p) n -> p k n", p=128), sk2, 0)
                WR = [(sk, 0), (sk, 1)]
                if hf == 0:
                    memset(uext[:, :, 0:15], 0.0, [('uext', sl)])
                else:
                    cp(uext[:, :, 0:15], halo[:, l, 2 * g:2 * g + 2, :], ['halo'], [('uext', sl)])
                    for oc in range(2):
                        cp(uext[:, oc, SB_:SB_ + 76].rearrange("p (b t) -> p b t", t=19)[:, :, 0:15],
                           sprevT[:, l, 2 * g + oc, :].rearrange("p (b t) -> p b t", t=15), ['sprevT'], [('uext', sl)])
                for (c0, n) in chunks(hf):
                    samp = (n == NS)
                    for oc in range(2):
                        b_, bk = sbank()
                        mm(bank(b_)[:, 0:n], [(W_[:, kc, oc * 128:(oc + 1) * 128], hT[:, kc, c0:c0 + n]) for kc in range(8)], WR + [('hT', c0)], [bk])
                        if samp:
                            cp(uext[:, oc, SB_:SB_ + 76].rearrange("p (b t) -> p b t", t=19)[:, :, 15:19],
                               bank(b_)[:, 0:NS].rearrange("p (b t) -> p b t", t=4), [bk], [('uext', sl)], eng='act')
                            cp(usc[:, oc, :], bank(b_)[:, 0:NS], [bk], [('usc', sl)])
                        else:
                            cp(uext[:, oc, 15 + c0:15 + c0 + n], bank(b_)[:, 0:n], [bk], [('uext', sl)], eng='act')
                        b2, bk2 = sbank()
                        mm(bank(b2)[:, 0:n], [(W_[:, kc, 256 + oc * 128:256 + (oc + 1) * 128], hT[:, kc, c0:c0 + n]) for kc in range(8)],
                           WR + [('hT', c0)], [bk2])
                        act(sg[:, oc, c0:c0 + n], bank(b2)[:, 0:n], AF.Silu, [bk2], [('sg', sl)])
                        bgstep()
                ginfo[g] = (sk2, Wg)

            def emit_PM(g):
                sl = g % 2
                uext = uext2[:, sl]; sg = sg2[:, sl]; usc = usc2[:, sl]
                sk2, Wg = ginfo[g]
                if hf == 0:
                    cp(halo[:, l, 2 * g:2 * g + 2, :], uext[:, :, 1024:1039], [('uext', sl)], ['halo'])
                else:
                    b_, bk = sbank()
                    for oc in range(2):
                        tp(bank(b_)[0:15, oc * 128:(oc + 1) * 128], uext[:, oc, 1024:1039], identf[:], [('uext', sl), 'identf'], [bk])
                        tp(bank(b_)[0:NS, 256 + oc * 128:256 + (oc + 1) * 128], usc[:, oc, :], identf[:], [('usc', sl), 'identf'], [bk])
                    cp(pstP[0:15, g * 256:(g + 1) * 256], bank(b_)[0:15, 0:256], [bk], [('tokbuf', 0)])
                    cp(pstS[0:NS, g * 256:(g + 1) * 256], bank(b_)[0:NS, 256:512], [bk], [('tokbuf', 1)])
                w = POOL_W[g]
                for oc in range(2):
                    bufs = [uext[:, oc, :], tmpA[:, :], tmpB[:, :], tmpA[:, :], tmpB[:, :]]
                    names = [('uext', sl), 'tmpA', 'tmpB', 'tmpA', 'tmpB']
                    sh = 1
                    for st in range(g + 1):
                        src, dst = bufs[st], bufs[st + 1]
                        lo = 2 * sh - 1
                        tt(dst[:, lo:WE], src[:, lo:WE], src[:, lo - sh:WE - sh], ALU.add, [names[st]], [names[st + 1]])
                        sh *= 2
                    res, rname = bufs[g + 1], names[g + 1]
                    stt(pooled[:, oc, 0:HALF], res[:, 15:15 + HALF], 1.0 / w, uext[:, oc, 15:15 + HALF], ALU.mult, ALU.subtract,
                        [rname, ('uext', sl)], ['pooled'])
                    if hf == 0:
                        tt(tfix[:, 0:16], res[:, 15:31], rc[:, g, :], ALU.mult, [rname, 'rc'], ['tfix'])
                        tt(pooled[:, oc, 0:16], tfix[:, 0:16], uext[:, oc, 15:31], ALU.subtract, ['tfix', ('uext', sl)], ['pooled'])
                    else:
                        stt(pooled[:, oc, HALF:HALF + NS].rearrange("p (b t) -> p b t", t=4),
                            res[:, SB_:SB_ + 76].rearrange("p (b t) -> p b t", t=19)[:, :, 15:19], 1.0 / w,
                            uext[:, oc, SB_:SB_ + 76].rearrange("p (b t) -> p b t", t=19)[:, :, 15:19], ALU.mult, ALU.subtract,
                            [rname, ('uext', sl)], ['pooled'])
                for (c0, n) in chunks(hf):
                    for oc in range(2):
                        b_, bk = sbank()
                        mm(bank(b_)[:, 0:n], [(Wg[:, ic, oc * 128:(oc + 1) * 128], pooled[:, ic, c0:c0 + n]) for ic in range(2)],
                           [(sk2, 0), 'pooled'], [bk])
                        stt(mixT[:, 2 * g + oc, c0:c0 + n], bank(b_)[:, 0:n], gfm[:, G_PS + l * 8 + 2 * g + oc:G_PS + l * 8 + 2 * g + oc + 1],
                            sg[:, oc, c0:c0 + n], ALU.mult, ALU.mult, [bk, ('sg', sl), 'gfm'], [('mixT', c0)])
                        bgstep(2)
            emit_U(0)
            for g in range(4):
                if g + 1 < 4:
                    emit_U(g + 1)
                emit_PM(g)
            if hf == 1:
                P.dma('sp', poolp_d[l], pstP[0:15, :], R=[('tokbuf', 0)], W=[('poolp', l)])
                for b in range(NSB):
                    P.dma('sp', pools_d[l, b, 11:15, :], pstS[b * 4:(b + 1) * 4, :], R=[('tokbuf', 1)], W=[('pools', l, b)])
            bgstep(100000)
            mem_attn_samp(hf, l)
            w_out(hf, l)

        def kv_phase(hf):
            norm_h(hf, G_KV)
            s, sk = wslot()
            W_ = wsl[:, s, 0:2560].rearrange("p (k n) -> p k n", n=320)
            loadw(W_, wkv_d.rearrange("(k p) n -> p k n", p=128), sk, 0)

            def kv_tile(np_, lhs_cols, c0, cos_ap, sin_ap, outs):
                b_, bk = sbank()
                mm(bank(b_)[0:np_, 0:320], [(hT[:, kc, lhs_cols], W_[:, kc, :]) for kc in range(8)], [(sk, 0), ('hT', c0)], [bk])
                ps = bank(b_)
                ks = nxt('kvtok', 2)
                kt = kvtok[0:np_, ks, :]
                act(sqb[0:np_, 0, 0:256], ps[0:np_, 0:256], AF.Square, [bk], [('sqb', 0), 'sm0'], accum=small[0:np_, 0:1])
                act(small[0:np_, 1:2], small[0:np_, 0:1], AF.Ln, ['sm0'], ['sm1'], scale=1.0 / 256, bias=EPS)
                act(small[0:np_, 2:3], small[0:np_, 1:2], AF.Exp, ['sm1'], ['sm2'], scale=-0.5)
                stt(kt[:, 0:256], ps[0:np_, 0:256], small[0:np_, 2:3], glat[0:np_, :], ALU.mult, ALU.mult, [bk, 'sm2', 'glat'], [('kvtok', ks)])
                x1 = ps[0:np_, 256:288]; x2 = ps[0:np_, 288:320]
                tt(rtmp[0:np_, 0, :], x1, cos_ap, ALU.mult, [bk] + COSK, ['rt0'])
                tt(rtmp[0:np_, 1, :], x2, sin_ap, ALU.mult, [bk] + COSK, ['rt1'])
                tt(rtmp[0:np_, 2, :], x1, sin_ap, ALU.mult, [bk] + COSK, ['rt2'])
                tt(rtmp[0:np_, 3, :], x2, cos_ap, ALU.mult, [bk] + COSK, ['rt3'])
                tt(kt[:, 256:288], rtmp[0:np_, 0, :], rtmp[0:np_, 1, :], ALU.subtract, ['rt0', 'rt1'], [('kvtok', ks)])
                tt(kt[:, 288:320], rtmp[0:np_, 2, :], rtmp[0:np_, 3, :], ALU.add, ['rt2', 'rt3'], [('kvtok', ks)])
                P.dma('sp', outs[0], kt[:, 0:256], R=[('kvtok', ks)], W=[('kvo', nxt('kvo', 1 << 30))])
                P.dma('sp', outs[1], kt[:, 256:320], R=[('kvtok', ks)], W=[('kvo', nxt('kvo', 1 << 30))])
                return kt, ks

            for t8 in range(8):
                gt = hf * 8 + t8
                kt, ks = kv_tile(128, slice(t8 * 128, (t8 + 1) * 128), (t8 // 4) * 512, cosk[:, t8, :], sink[:, t8, :],
                                 (ckvp_d[gt * 128:(gt + 1) * 128, :], krp_d[gt * 128:(gt + 1) * 128, :]))
                cp(cnat[:, gt, :], kt[:, 0:256], [('kvtok', ks)], ['cnat'])
                cp(krb[:], kt[:, 256:320], [('kvtok', ks)], ['krb'])
                for cc in range(2):
                    tp(psB[:, cc * 128:(cc + 1) * 128], cnat[:, gt, cc * 128:(cc + 1) * 128], identb[:], ['cnat', 'identb'], ['psB'])
                tp(psB[0:64, 256:384], krb[:], identb[:], ['krb', 'identb'], ['psB'])
                cp(ckvT[:, :, gt * 128:(gt + 1) * 128], psB[:, 0:256].rearrange("p (a b) -> p a b", b=128), ['psB'], ['ckvT'])
                cp(krT[:, gt * 128:(gt + 1) * 128], psB[0:64, 256:384], ['psB'], ['krT'])
            if hf == 1:
                for b in range(NSB):
                    kt, ks = kv_tile(4, slice(HALF + b * 4, HALF + (b + 1) * 4), HALF, cosks[:], sinks[:],
                                     (ckvs_d[b * 4:(b + 1) * 4, :], krs_d[b * 4:(b + 1) * 4, :]))
                    cp(cnat_n[0:4, b, 1:257], kt[:, 0:256], [('kvtok', ks)], ['cnat_n'])
                    cp(krb[0:4, :], kt[:, 256:320], [('kvtok', ks)], ['krb'])
                    for cc in range(2):
                        tp(psB[:, cc * 4:(cc + 1) * 4], cnat_n[0:4, b, 1 + cc * 128:1 + (cc + 1) * 128], identb[0:4, 0:4], ['cnat_n', 'identb'], ['psB'])
                    tp(psB[0:64, 8:12], krb[0:4, :], identb[0:4, 0:4], ['krb', 'identb'], ['psB'])
                    cp(ckvT_n[:, :, b, :], psB[:, 0:8].rearrange("p (a b) -> p a b", b=4), ['psB'], ['ckvT_n'])
                    cp(krT_n[:, b, :], psB[0:64, 8:12], ['psB'], ['krT_n'])

        def layer_b(hf, j, B):
            cqn, cosq, sinq, qnT, qrT, t1, t2, qlT, pT, olT, sgB, QLs, QRs, sgs, pgb, ckvTs, krTs, pTs, ols, olTs = B
            l = 2 + j
            norm_h(hf, G_NORM + l * 8, do_rstd=(j == 1))
            wv_ = w_in_view(l)
            wq = wqup_d[j].rearrange("(k p) n -> p k n", p=128)
            s, sk = wslot()
            Wc = wsl[:, s, 0:3072].rearrange("p (k n) -> p k n", n=384)
            loadw(Wc, wv_[:, :, 0:384], sk, 0)
            for (c0, n) in chunks(hf):
                for cc in range(3):
                    mm(bank(cc)[:, 0:n], [(Wc[:, kc, cc * 128:(cc + 1) * 128], hT[:, kc, c0:c0 + n]) for kc in range(8)],
                       [(sk, 0), ('hT', c0)], [('ps', cc)])
                for cc in range(3):
                    s_ = nxt('sq', 2)
                    act(sqb[:, s_, 0:n], bank(cc)[:, 0:n], AF.Square, [('ps', cc)], [('sqb', s_)])
                    mm(bank(3)[:, 0:n], [(onesb[:], sqb[:, s_, 0:n])], [('sqb', s_), 'onesb'], [('ps', 3)], start=(cc == 0), stop=(cc == 2))
                act(rden[:, 0:n], bank(3)[:, 0:n], AF.Ln, [('ps', 3)], ['rden'], scale=1.0 / 384, bias=EPS)
                act(rden[:, 0:n], rden[:, 0:n], AF.Exp, ['rden'], ['rden'], scale=-0.5)
                for cc in range(3):
                    stt(cqn[:, cc, c0:c0 + n], bank(cc)[:, 0:n], gfm[:, G_Q + j * 3 + cc:G_Q + j * 3 + cc + 1], rden[:, 0:n], ALU.mult, ALU.mult,
                        [('ps', cc), 'rden', 'gfm'], ['cqn'])
            cflat = cache_d

            def sample_gen():
                NG = NPG // 2
                for b in range(NSB):
                    ginfo = {}

                    def emit_G(g, b=b):
                        sg_ = nxt('pgg', 4)
                        for k in range(2):
                            col = b * NPG + g * 2 + k
                            P.dma('pool', pgb[:, sg_ * 2 + k, 1:321], cflat[:, :], R=['idx'], W=[('pgb', sg_ * 2 + k)],
                                  indirect=bass.IndirectOffsetOnAxis(idx[:, col:col + 1], 0))
                        ginfo[g] = [sg_]

                    def emit_T(g):
                        sg_ = ginfo[g][0]
                        pbk = 'psB'
                        ts_ = nxt('ckvTs', 2)
                        for k in range(2):
                            sl = sg_ * 2 + k
                            for cc in range(2):
                                tp(psB[:, (k * 2 + cc) * 128:(k * 2 + cc + 1) * 128], pgb[:, sl, 1 + cc * 128:1 + (cc + 1) * 128], identb[:],
                                   [('pgb', sl), 'identb'], [pbk])
                            tp(psB[0:64, 512 + k * 128:512 + (k + 1) * 128], pgb[:, sl, 257:321], identb[:], [('pgb', sl), 'identb'], [pbk])
                        cp(ckvTs[:, ts_ * 2:ts_ * 2 + 2, :, :], psB[:, 0:512].rearrange("p (k c s) -> p k c s", c=2, s=128), [pbk], [('ckvTs', ts_)])
                        cp(krTs[:, ts_ * 2:ts_ * 2 + 2, :], psB[0:64, 512:768].rearrange("p (k s) -> p k s", s=128), [pbk], [('krTs', ts_)], eng='act')
                        ginfo[g].append(ts_)

                    def emit_S(g, b=b):
                        sg_, ts_ = ginfo[g]
                        sb_, sbk = sbank()
                        for k in range(2):
                            sl = ts_ * 2 + k
                            mm(bank(sb_)[:, k * 32:(k + 1) * 32], [(ckvTs[:, sl, 0, :], QLs[:, 0, b, :, :]), (ckvTs[:, sl, 1, :], QLs[:, 1, b, :, :]),
                                                                  (krTs[:, sl, :], QRs[:, b, :, :])], [('ckvTs', ts_), ('krTs', ts_), 'QLs', 'QRs'], [sbk])
                        pp = nxt('pTs', 2)
                        act(pTs[:, pp * 2:pp * 2 + 2, :], bank(sb_)[:, 0:64].rearrange("p (k q) -> p k q", q=32), AF.Exp, [sbk], [('pTs', pp)], scale=MLA_SCALE)
                        ginfo[g].append(pp)

                    def emit_PV(g):
                        sg_, ts_, pp = ginfo[g]
                        for k in range(2):
                            mm(bank(6)[0:32, 0:257], [(pTs[:, pp * 2 + k, :], pgb[:, sg_ * 2 + k, 0:257])], [('pTs', pp), ('pgb', sg_ * 2 + k)], [('ps', 6)],
                               start=(g == 0 and k == 0), stop=False)
                    emit_G(0); emit_G(1); emit_G(2)
                    emit_T(0)
                    yield
                    for g in range(NG):
                        if g + 1 < NG:
                            emit_T(g + 1)
                        emit_S(g)
                        if g >= 1:
                            emit_PV(g - 1)
                        if g + 3 < NG:
                            emit_G(g + 3)
                        yield
                    emit_PV(NG - 1)
                    sb_, sbk = sbank()
                    mm(bank(sb_)[0:4, 0:32], [(ckvT_n[:, 0, b, :], QLs[:, 0, b, :, :]), (ckvT_n[:, 1, b, :], QLs[:, 1, b, :, :]),
                                              (krT_n[:, b, :], QRs[:, b, :, :])], ['ckvT_n', 'krT_n', 'QLs', 'QRs'], [sbk])
                    act(pTn[0:4, :], bank(sb_)[0:4, 0:32], AF.Exp, [sbk], ['pTn'], scale=MLA_SCALE)
                    tt(pTn[0:4, :], pTn[0:4, :], tri4[:], ALU.mult, ['pTn', 'tri4'], ['pTn'])
                    mm(bank(6)[0:32, 0:257], [(pTn[0:4, :], cnat_n[0:4, b, :])], ['pTn', 'cnat_n'], [('ps', 6)], start=False, stop=True)
                    recip(small[0:32, 3:4], bank(6)[0:32, 0:1], [('ps', 6)], ['sm3'])
                    P.op('dve', lambda e: e.tensor_scalar(out=ols[:], in0=bank(6)[0:32, 1:257], scalar1=small[0:32, 3:4], scalar2=None, op0=ALU.mult),
                         [('ps', 6), 'sm3'], ['ols'])
                    for cc in range(2):
                        tp(psB[:, cc * 32:(cc + 1) * 32], ols[:, cc * 128:(cc + 1) * 128], identb[0:32, 0:32], ['ols', 'identb'], ['psB'])
                    cp(olTs[:, :, b, :], psB[:, 0:64].rearrange("p (a b) -> p a b", b=32), ['psB'], ['olTs'])
                    yield
            allc = list(enumerate(chunks(hf)))
            if hf == 1:
                passes = [[c for c in allc if c[1][1] == NS], [c for c in allc if c[1][1] != NS]]
            else:
                passes = [allc]
            gen_box = [None]
            mgen = mem_attn_gen(hf, l, 384 + 1024, 384 + 1024 + 512)
            mcnt = [0]

            def step(k):
                if k < 1000:
                    mcnt[0] += 1
                    if mcnt[0] % 2 == 0:
                        next(mgen, None)
                else:
                    for _ in mgen:
                        pass
                for _ in range(k):
                    if gen_box[0] is not None:
                        if next(gen_box[0], 'done') == 'done':
                            gen_box[0] = None
            for pi, clist in enumerate(passes):
                if hf == 1 and pi == 1:
                    gen_box[0] = sample_gen()
                for h in range(8):
                    s, sk = wslot()
                    Wq = wsl[:, s, 0:768].rearrange("p (k n) -> p k n", n=256)
                    loadw(Wq[:, :, 0:192], wq[:, :, h * 192:(h + 1) * 192], sk, 0)
                    loadw(Wq[:, :, 192:224], wq[:, :, h * 192 + 160:h * 192 + 192], sk, 1)
                    loadw(Wq[:, :, 224:256], wq[:, :, h * 192 + 128:h * 192 + 160], sk, 2)
                    Wk = wsl[:, s, 768:1024]
                    loadw(Wk, wkT_d[:, h, :], sk, 3)
                    Wv = wsl[:, s, 1024:1280].rearrange("p (c v) -> p c v", v=128)
                    loadw(Wv, wv_d.rearrange("(c p) n -> p c n", p=128)[:, :, h * 128:(h + 1) * 128], sk, 4)
                    Wg = wsl[:, s, 2048:3072].rearrange("p (k n) -> p k n", n=128)
                    loadw(Wg, wv_[:, :, 384 + h * 128:384 + (h + 1) * 128], sk, 5)
                    for ci, (c0, n) in clist:
                        samp = (n == NS)
                        b_, bk = sbank()
                        mm(bank(b_)[:, 0:n], [(Wq[:, kc, 0:128], cqn[:, kc, c0:c0 + n]) for kc in range(3)], [(sk, 0), 'cqn'], [bk])
                        cp(qnT[:, 0:n], bank(b_)[:, 0:n], [bk], ['qnT'], eng='act')
                        b1, bk1 = sbank()
                        mm(bank(b1)[0:64, 0:n], [(Wq[:, kc, 128:192], cqn[:, kc, c0:c0 + n]) for kc in range(3)], [(sk, 0), 'cqn'], [bk1])
                        b2, bk2 = sbank()
                        mm(bank(b2)[0:64, 0:n], [(Wq[:, kc, 192:256], cqn[:, kc, c0:c0 + n]) for kc in range(3)], [(sk, 1), (sk, 2), 'cqn'], [bk2])
                        tt(t1[:, 0:n], bank(b1)[0:64, 0:n], cosq[:, c0:c0 + n], ALU.mult, [bk1, 'cosq', 'sinq'], ['tsb'])
                        tt(t2[:, 0:n], bank(b2)[0:64, 0:n], sinq[:, c0:c0 + n], ALU.mult, [bk2, 'cosq', 'sinq'], ['rden'])
                        if samp:
                            tt(QRs[:, :, h, :], t1[:, 0:NS].rearrange("p (b t) -> p b t", t=4), t2[:, 0:NS].rearrange("p (b t) -> p b t", t=4),
                               ALU.add, ['tsb', 'rden'], ['QRs'])
                        else:
                            tt(qrT[:, 0:n], t1[:, 0:n], t2[:, 0:n], ALU.add, ['tsb', 'rden'], ['qrT'])
                        for cc in range(2):
                            b3, bk3 = sbank()
                            mm(bank(b3)[:, 0:n], [(Wk[:, cc * 128:(cc + 1) * 128], qnT[:, 0:n])], [(sk, 3), 'qnT'], [bk3])
                            if samp:
                                cp(QLs[:, cc, :, h, :], bank(b3)[:, 0:NS].rearrange("p (b t) -> p b t", t=4), [bk3], ['QLs'], eng='act')
                            else:
                                cp(qlT[:, cc, 0:n], bank(b3)[:, 0:n], [bk3], ['qlT'], eng='act')
                        b4, bk4 = sbank()
                        mm(bank(b4)[:, 0:n], [(Wg[:, kc, :], hT[:, kc, c0:c0 + n]) for kc in range(8)], [(sk, 5), ('hT', c0)], [bk4])
                        if samp:
                            act(sgs[:, h, :], bank(b4)[:, 0:n], AF.Silu, [bk4], ['sgs'])
                            continue
                        act(sgB[:, 0:n], bank(b4)[:, 0:n], AF.Silu, [bk4], ['sgB'])
                        gtc = hf * 2 + ci
                        last = 4 * gtc + 3
                        tinfo = {}

                        def emit_S(i, gtc=gtc):
                            r = i - 4 * gtc
                            cs = 128 * r if r > 0 else 0
                            sb_, sbk = sbank()
                            mm(bank(sb_)[:, cs:512], [(ckvT[:, 0, i * 128:(i + 1) * 128], qlT[:, 0, cs:512]),
                                                      (ckvT[:, 1, i * 128:(i + 1) * 128], qlT[:, 1, cs:512]),
                                                      (krT[:, i * 128:(i + 1) * 128], qrT[:, cs:512])], ['ckvT', 'krT', 'qlT', 'qrT'], [sbk])
                            ps_ = nxt('pT', 3)
                            act(pT[:, ps_, cs:512], bank(sb_)[:, cs:512], AF.Exp, [sbk], [('pT', ps_)], scale=MLA_SCALE)
                            if r >= 0:
                                tt(pT[:, ps_, cs:cs + 128], pT[:, ps_, cs:cs + 128], tri[:], ALU.mult, [('pT', ps_), 'tri'], [('pT', ps_)])
                            if i == 0:
                                cp(dacc[:, cs:512], pT[:, ps_, cs:512], [('pT', ps_)], ['dacc'])
                            else:
                                tt(dacc[:, cs:512], dacc[:, cs:512], pT[:, ps_, cs:512], ALU.add, [('pT', ps_), 'dacc'], ['dacc'])
                            tinfo[i] = (cs, ps_)

                        def emit_PV(i, last=last):
                            cs, ps_ = tinfo[i]

                            def pv(e):
                                e.matmul(bank(3)[:, cs:512], cnat[:, i, 0:128], pT[:, ps_, cs:512], start=(i == 0), stop=(i == last))
                                return e.matmul(bank(4)[:, cs:512], cnat[:, i, 128:256], pT[:, ps_, cs:512], start=(i == 0), stop=(i == last))
                            P.op('pe', pv, ['cnat', ('pT', ps_)], [('ps', 3), ('ps', 4)])
                        emit_S(0)
                        for i in range(last + 1):
                            if i + 1 <= last:
                                emit_S(i + 1)
                            emit_PV(i)
                            step(1 + (1 if i % 4 == 0 else 0))
                        mm(bank(5), [(onesf[:], dacc[:])], ['dacc', 'onesf'], [('ps', 5)])
                        act(rden[:], bank(5), AF.Ln, [('ps', 5)], ['rden'])
                        act(rden[:], rden[:], AF.Exp, ['rden'], ['rden'], scale=-1.0)
                        tt(olT[:, 0, :], bank(3), rden[:], ALU.mult, [('ps', 3), 'rden'], [('olT', 0)])
                        tt(olT[:, 1, :], bank(4), rden[:], ALU.mult, [('ps', 4), 'rden'], [('olT', 1)])
                        b5, bk5 = sbank()
                        mm(bank(b5), [(Wv[:, cc, :], olT[:, cc, :]) for cc in range(2)], [(sk, 4), ('olT', 0), ('olT', 1)], [bk5])
                        tt(mixT[:, h, c0:c0 + n], bank(b5), sgB[:], ALU.mult, [bk5, 'sgB'], [('mixT', c0)])
            step(100000)
            if hf == 1:
                s, sk = wslot()
                Wva = wsl[:, s, 0:2048].rearrange("p (c n) -> p c n", n=1024)
                loadw(Wva, wv_d.rearrange("(c p) n -> p c n", p=128), sk, 0)
                for b in range(NSB):
                    b5, bk5 = sbank()
                    for h in range(8):
                        mm(bank(b5)[:, h * 4:(h + 1) * 4], [(Wva[:, cc, h * 128:(h + 1) * 128], olTs[:, cc, b, h * 4:(h + 1) * 4]) for cc in range(2)],
                           [(sk, 0), 'olTs'], [bk5])
                    tt(mixT[:, 0:8, HALF + b * 4:HALF + (b + 1) * 4], bank(b5)[:, 0:32].rearrange("p (h t) -> p h t", t=4),
                       sgs[:, :, b * 4:(b + 1) * 4], ALU.mult, [bk5, 'sgs'], [('mixT', HALF)])
            mem_attn_samp(hf, l)
            w_out(hf, l)

        def load_x(hf):
            for t8 in range(8):
                s = nxt('tokbuf', 2)
                P.dma('sp', tokbuf[:, s, :], x_d[hf * HALF + t8 * 128:hf * HALF + (t8 + 1) * 128, :], W=[('tokbuf', s)])
                for q in range(2):
                    b_, bk = sbank()
                    for k4 in range(4):
                        kc = q * 4 + k4
                        tp(bank(b_)[:, k4 * 128:(k4 + 1) * 128], tokbuf[:, s, kc * 128:(kc + 1) * 128], identf[:], [('tokbuf', s), 'identf'], [bk])
                    cp(xT[:, q * 4:(q + 1) * 4, t8 * 128:(t8 + 1) * 128], bank(b_).rearrange("p (a b) -> p a b", b=128), [bk], ['xT'],
                       eng=('act' if q else 'dve'))
            if hf == 1:
                s = nxt('tokbuf', 2)
                P.dma('sp', tokbuf[0:NS, s, :], xs_d[:, :], W=[('tokbuf', s)])
                for q in range(2):
                    b_, bk = sbank()
                    for k4 in range(4):
                        kc = q * 4 + k4
                        tp(bank(b_)[:, k4 * NS:(k4 + 1) * NS], tokbuf[0:NS, s, kc * 128:(kc + 1) * 128], identf[0:NS, 0:NS], [('tokbuf', s), 'identf'], [bk])
                    cp(xT[:, q * 4:(q + 1) * 4, HALF:HALF + NS], bank(b_)[:, 0:4 * NS].rearrange("p (a b) -> p a b", b=NS), [bk], ['xT'])

        def final(hf):
            def tile(np_, cols, out_ap):
                for q in range(2):
                    for k4 in range(4):
                        kc = q * 4 + k4
                        tp(psF[0:np_, 4 + q, k4 * 128:(k4 + 1) * 128], xT[:, kc, cols], identf[:], ['xT', 'identf'], [('ps', 4 + q)])
                s = nxt('tokbuf', 2)
                ps2 = psF[0:np_, 4:6, :]
                act(sqb[0:np_, :, 0:512], ps2, AF.Square, [('ps', 4), ('ps', 5)], [('sqb', 0), ('sqb', 1), 'sm0'], accum=small[0:np_, 0:1])
                act(small[0:np_, 1:2], small[0:np_, 0:1], AF.Ln, ['sm0'], ['sm1'], scale=1.0 / D, bias=EPS)
                act(small[0:np_, 2:3], small[0:np_, 1:2], AF.Exp, ['sm1'], ['sm2'], scale=-0.5)
                stt(tokbuf[0:np_, s, :].rearrange("p (a b) -> p a b", b=512), ps2, small[0:np_, 2:3],
                    gfin[0:np_, :].rearrange("p (a b) -> p a b", b=512), ALU.mult, ALU.mult,
                    [('ps', 4), ('ps', 5), 'sm2', 'gfin'], [('tokbuf', s)])
                P.dma('sp', out_ap, tokbuf[0:np_, s, :], R=[('tokbuf', s)], W=[('yout', nxt('yout', 1 << 30))])
            for t8 in range(8):
                g0 = hf * HALF + t8 * 128
                tile(128, slice(t8 * 128, (t8 + 1) * 128), y_d[g0:g0 + 128, :])
            if hf == 1:
                tile(NS, slice(HALF, HALF + NS), ys_d[:, :])

        for hf in range(2):
            load_x(hf)
            stage('loadx%d' % hf)
            with ExitStack() as as_:
                WEm = 1039 + NSB * 19
                A = (sb("uext", [128, 2, 2, WEm], F32, as_), sb("tmpA", [128, WEm], F32, as_), sb("tmpB", [128, WEm], F32, as_),
                     sb("pooled", [128, 2, NH], BF16, as_), sb("sg", [128, 2, 2, NH], BF16, as_), sb("usc", [128, 2, 2, NS], F32, as_),
                     sb("tfix", [128, 16], F32, as_), sb("sprevT", [128, 2, 8, 60], F32, as_))
                if hf == 1:
                    load_sprev(A[7])
                rc = sb("rc", [128, 4, 16], F32, as_)
                P.dma('sp', rc[:], rc_d[:], W=['rc'])
                for l in range(2):
                    layer_a(hf, l, A)
                    stage('a%d_%d' % (hf, l))
                P.barrier()
                sbytes[0] -= 0
            with ExitStack() as bs_:
                cosk = sb("cosk", [128, 8, 32], F32, bs_); sink = sb("sink", [128, 8, 32], F32, bs_)
                kvtok = sb("kvtok", [128, 2, 320], F32, bs_); krb = sb("krb", [128, 64], BF16, bs_); rtmp = sb("rtmp", [128, 4, 32], F32, bs_)
                gfin = sb("gfin", [128, D], F32, bs_)
                P.dma('sp', gfin[:], gfin_d[0:1, :].to_broadcast([128, D]), W=['gfin'])
                cosks = sb("cosks", [4, 32], F32, bs_); sinks = sb("sinks", [4, 32], F32, bs_)
                B = (sb("cqn", [128, 3, NH], BF16, bs_), sb("cosq", [64, NH], F32, bs_), sb("sinq", [64, NH], F32, bs_),
                     sb("qnT", [128, 512], BF16, bs_), sb("qrT", [64, 512], BF16, bs_), tsb[0:64, :], rden[0:64, :],
                     sb("qlT", [128, 2, 512], BF16, bs_), sb("pT", [128, 3, 512], BF16, bs_), sb("olT", [128, 2, 512], BF16, bs_),
                     sb("sgB", [128, 512], BF16, bs_), sb("QLs", [128, 2, NSB, 8, NST], BF16, bs_), sb("QRs", [64, NSB, 8, NST], BF16, bs_),
                     sb("sgs", [128, 8, NS], BF16, bs_), sb("pgb", [128, 8, 321], BF16, bs_), sb("ckvTs", [128, 4, 2, 128], BF16, bs_),
                     sb("krTs", [64, 4, 128], BF16, bs_), sb("pTs", [128, 4, 32], BF16, bs_), sb("ols", [32, 256], BF16, bs_),
                     sb("olTs", [128, 2, NSB, 32], BF16, bs_))
                P.dma('sp', cosk[:], cosk_d[:, hf * 8:(hf + 1) * 8, :], W=['cosk']); P.dma('sp', sink[:], sink_d[:, hf * 8:(hf + 1) * 8, :], W=['sink'])
                P.dma('sp', cosks[:], cosks_d[:], W=['cosks']); P.dma('sp', sinks[:], sinks_d[:], W=['sinks'])
                ncol = HALF + (NS if hf else 0)
                P.dma('sp', B[1][:, 0:ncol], cosq_d[:, hf * HALF:hf * HALF + ncol], W=['cosq'])
                P.dma('sp', B[2][:, 0:ncol], sinq_d[:, hf * HALF:hf * HALF + ncol], W=['sinq'])
                memset(B[14][:, :, 0:1], 1.0, [('pgb', i) for i in range(8)])
                pTn = sb("pTn", [4, 32], BF16, bs_)
                dacc = sb("dacc", [128, 512], F32, bs_)
                onesf = sb("onesf", [128, 128], F32, bs_)
                memset(onesf[:], 1.0, ['onesf'])
                kv_phase(hf)
                stage('kv%d' % hf)
                for j in range(2):
                    layer_b(hf, j, B)
                    stage('b%d_%d' % (hf, j))
                final(hf)
                stage('final%d' % hf)
                P.barrier()

    except _Stop:
        P.finish()
        return nc
    P.finish()
    es.close()
    return nc


_NC = None


def _consts():
    half = 32
    inv = (10000.0 ** (-np.arange(half, dtype=np.float32) / half)).astype(np.float32)
    pos = np.concatenate([np.arange(SEQ), np.tile(16384 + np.arange(NST), NSB)]).astype(np.float32)
    ang = pos[None, :] * inv[:, None]
    cos = np.cos(ang).astype(np.float32); sin = np.sin(ang).astype(np.float32)
    cosq = np.concatenate([cos, cos], 0); sinq = np.concatenate([-sin, sin], 0)
    angk = (np.arange(SEQ, dtype=np.float32)[:, None] * inv[None, :])
    cosk = np.cos(angk).astype(np.float32).reshape(16, 128, 32).transpose(1, 0, 2).copy()
    sink = np.sin(angk).astype(np.float32).reshape(16, 128, 32).transpose(1, 0, 2).copy()
    angs = ((16384 + np.arange(NST)).astype(np.float32)[:, None] * inv[None, :])
    cosks = np.cos(angs).astype(np.float32); sinks = np.sin(angs).astype(np.float32)
    tri = (np.arange(128)[None, :] >= np.arange(128)[:, None]).astype(np.float32)
    tri4 = np.tile((np.arange(4)[None, :] >= np.arange(4)[:, None]).astype(np.float32), (1, 8))
    rc = np.zeros((128, 4, 16), np.float32)
    for g, w in enumerate(POOL_W):
        rc[:, g, :] = 1.0 / np.minimum(np.arange(16) + 1, w)
    return dict(identf=np.eye(128, dtype=np.float32), tri=tri, tri4=tri4, cosq=cosq, sinq=sinq, cosk=cosk, sink=sink,
                cosks=cosks, sinks=sinks, rc=rc, iop=np.arange(128, dtype=np.float32)[:, None].copy())


def kernel(x_prompt, x_sample, state_pool, cache_ckv, cache_krope, cache_mem_k, cache_mem_v,
           page_table, mem_prompt, g_norm, w_in_a, w_pool_grp, pool_scale, w_in_b, g_q_latent,
           w_q_up, g_kv_in, w_kv_down, g_kv_latent, w_k_up, w_v_up, g_mem, w_mem_k, w_mem_v,
           w_out, g_final):
    global _NC
    f = lambda a: np.ascontiguousarray(np.asarray(a, dtype=np.float32))
    if _NC is None:
        _NC = build()
    nc = _NC

    def fm(v, n):
        return np.asarray(v, np.float32).reshape(n, 128).T
    gfm = np.zeros((128, 96), np.float32)
    gfm = np.concatenate([fm(np.asarray(g_norm).reshape(-1), 32), fm(g_kv_in, 8), fm(np.asarray(g_mem).reshape(-1), 32),
                          fm(np.asarray(pool_scale).reshape(-1), 16), fm(np.asarray(g_q_latent).reshape(-1), 6)], axis=1)
    gfm = np.ascontiguousarray(np.pad(gfm, ((0, 0), (0, 96 - gfm.shape[1]))))
    cache = np.ascontiguousarray(np.concatenate([np.asarray(cache_ckv, np.float32), np.asarray(cache_krope, np.float32)], axis=2).reshape(5120 * 128, 320))
    if CACHE_ROWS != 5120 * 128:
        cache = np.ascontiguousarray(cache[:CACHE_ROWS])
    shared = dict(cache=cache, gfm=f(gfm), wina=f(w_in_a), wgrp=f(w_pool_grp), winb=f(w_in_b), wqup=f(w_q_up), wkv=f(w_kv_down),
                  glat=f(g_kv_latent).reshape(1, 256), gfin=f(g_final).reshape(1, D),
                  wkT=f(np.asarray(w_k_up).transpose(2, 1, 0)), wv=f(np.asarray(w_v_up).reshape(256, 1024)),
                  wmk=f(w_mem_k), wmv=f(w_mem_v), wout=f(w_out))
    shared.update(_consts())
    in_maps = []
    for c in range(NCORES):
        m = dict(shared)
        m['x'] = f(x_prompt[c]); m['xs'] = f(np.asarray(x_sample)[4 * c:4 * c + 4].reshape(NS, D))
        m['sp'] = f(np.asarray(state_pool)[:, 4 * c:4 * c + 4].reshape(2, 60, D))
        m['cmk'] = f(np.asarray(cache_mem_k)[:, 4 * c:4 * c + 4].reshape(4, 4, 256, 512))
        m['cmv'] = f(np.asarray(cache_mem_v)[:, 4 * c:4 * c + 4].reshape(4, 4, 256, 512))
        m['pt'] = np.ascontiguousarray(np.asarray(page_table, np.int32)[4 * c:4 * c + 4].reshape(1, NSB * NPG))
        m['memp'] = f(mem_prompt[c])
        in_maps.append(m)
    res = run_bass_kernel_spmd(nc, in_maps, core_ids=list(range(NCORES)))
    R = res.results
    cat = lambda k: np.stack([R[c][k] for c in range(NCORES)], 0)
    y_p = cat('y'); y_s = cat('ys').reshape(-1, 4, D)
    pool_p = cat('poolp').transpose(1, 0, 2, 3).copy()
    pool_s = cat('pools').transpose(1, 0, 2, 3, 4).reshape(2, -1, 15, D).copy()
    ckv_p = cat('ckvp'); kr_p = cat('krp')
    ckv_s = cat('ckvs').reshape(-1, 4, 256); kr_s = cat('krs').reshape(-1, 4, 64)
    mk_p = cat('mkp').transpose(1, 0, 2, 3).reshape(4, -1, 256, 4, 128).copy()
    mv_p = cat('mvp').transpose(1, 0, 2, 3).reshape(4, -1, 256, 4, 128).copy()
    return (y_p, y_s, pool_p, pool_s, ckv_p, kr_p, ckv_s, kr_s, mk_p, mv_p)
```
